# Optimizing a Trainium2 kernel written in Bass

```python
import jax, jax.numpy as jnp
from jax import lax
import numpy as np

D_MODEL = 1024
BATCH = 8
SEQ = 4096
DEPTH = 4

N_MIXERS = 2
N_A = (DEPTH + 1) // 2
N_B = DEPTH // 2

D_RNN = 3 * D_MODEL // 2
LRU_HEADS = 12
LRU_BW = D_RNN // LRU_HEADS
CONV_WIDTH = 4
LRU_C = 8.0

D_POOL = D_MODEL
POOL_WINDOWS = (2, 4, 8, 16)
POOL_GROUPS = len(POOL_WINDOWS)
POOL_GW = D_POOL // POOL_GROUPS

D_FF = 4 * D_MODEL
PLE_DIM = 256
ALPHA = (2 * DEPTH) ** 0.25
BETA = (8 * DEPTH) ** (-0.25)
LN_EPS = 1e-5

kernel_name = "hybrid_rglru_pool_deepnorm_trunk"


def layer_norm(x, g, b):
    xf = x.astype(jnp.float32)
    mu = jnp.mean(xf, axis=-1, keepdims=True)
    var = jnp.mean(jnp.square(xf - mu), axis=-1, keepdims=True)
    y = (xf - mu) * lax.rsqrt(var + LN_EPS)
    return (y * g.astype(jnp.float32) + b.astype(jnp.float32)).astype(x.dtype)


def causal_depthwise_conv(u, w, b):
    s = u.shape[1]
    up = jnp.pad(u, ((0, 0), (CONV_WIDTH - 1, 0), (0, 0)))
    out = b
    for k in range(CONV_WIDTH):
        out = out + up[:, k:k + s] * w[k]
    return out


def _lin_rec_combine(left, right):
    a1, b1 = left
    a2, b2 = right
    return a1 * a2, a2 * b1 + b2


def rg_lru(u, wa, ba, wx, bx, lam):
    bsz, s, _ = u.shape
    uh = u.reshape(bsz, s, LRU_HEADS, LRU_BW)
    r = jax.nn.sigmoid(jnp.einsum('bshi,hij->bshj', uh, wa).reshape(bsz, s, D_RNN) + ba)
    ig = jax.nn.sigmoid(jnp.einsum('bshi,hij->bshj', uh, wx).reshape(bsz, s, D_RNN) + bx)
    log_a = -LRU_C * r.astype(jnp.float32) * jax.nn.softplus(-lam.astype(jnp.float32))
    a = jnp.exp(log_a)
    mult = jnp.sqrt(-jnp.expm1(2.0 * log_a))
    mult = mult.at[:, 0].set(1.0)
    bterm = mult * (ig * u).astype(jnp.float32)
    _, h = lax.associative_scan(_lin_rec_combine, (a, bterm), axis=1)
    return h.astype(u.dtype)


def recurrent_mixer(x, w_in, conv_w, conv_b, wa, ba, wx, bx, lam, w_out):
    proj = x @ w_in
    u, y = proj[..., :D_RNN], proj[..., D_RNN:]
    u = causal_depthwise_conv(u, conv_w, conv_b)
    h = rg_lru(u, wa, ba, wx, bx, lam)
    return (h * jax.nn.gelu(y)) @ w_out


def pooling_mixer(x, w_in, w_grp, b_grp, scale, w_out):
    u = x @ w_in
    s = u.shape[1]
    pos = jnp.arange(s, dtype=jnp.int32)
    outs = []
    for g, w in enumerate(POOL_WINDOWS):
        ug = u[..., g * POOL_GW:(g + 1) * POOL_GW].astype(jnp.float32)
        cs = jnp.cumsum(ug, axis=1)
        cs_prev = jnp.pad(cs, ((0, 0), (w, 0), (0, 0)))[:, :s]
        cnt = jnp.minimum(pos + 1, w).astype(jnp.float32)[None, :, None]
        outs.append((cs - cs_prev) / cnt - ug)
    z = jnp.stack(outs, axis=2).astype(u.dtype)
    z = jnp.einsum('bsgi,gij->bsgj', z, w_grp).reshape(u.shape) + b_grp
    return (z * scale) @ w_out


def sq_relu_mlp(x, w1, w2):
    h = jax.nn.relu(x @ w1)
    return (h * h) @ w2


def setup_inputs(seed: int = 0) -> dict:
    key = jax.random.key(seed)
    ks = jax.random.split(key, 26)
    f32 = jnp.float32

    def nrm(k, shape, scale):
        return jax.random.normal(k, shape, f32) * scale

    a_c = jax.random.uniform(ks[8], (N_A, D_RNN), f32, minval=0.9, maxval=0.999)
    a0 = a_c ** (1.0 / LRU_C)
    lam = jnp.log(a0) - jnp.log1p(-a0)
    return {
        "x": nrm(ks[0], (BATCH, SEQ, D_MODEL), 1.0),
        "p": nrm(ks[1], (DEPTH, BATCH, SEQ, PLE_DIM), 1.0),
        "lru_w_in": nrm(ks[2], (N_A, D_MODEL, 2 * D_RNN), D_MODEL ** -0.5),
        "lru_conv_w": nrm(ks[3], (N_A, CONV_WIDTH, D_RNN), CONV_WIDTH ** -0.5),
        "lru_conv_b": nrm(ks[4], (N_A, D_RNN), 0.01),
        "lru_wa": nrm(ks[5], (N_A, LRU_HEADS, LRU_BW, LRU_BW), LRU_BW ** -0.5),
        "lru_ba": nrm(ks[6], (N_A, D_RNN), 0.01),
        "lru_wx": nrm(ks[7], (N_A, LRU_HEADS, LRU_BW, LRU_BW), LRU_BW ** -0.5),
        "lru_bx": nrm(ks[9], (N_A, D_RNN), 0.01),
        "lru_lambda": lam,
        "lru_w_out": nrm(ks[10], (N_A, D_RNN, D_MODEL), BETA * D_RNN ** -0.5),
        "pool_w_in": nrm(ks[11], (N_B, D_MODEL, D_POOL), D_MODEL ** -0.5),
        "pool_w_grp": nrm(ks[12], (N_B, POOL_GROUPS, POOL_GW, POOL_GW), POOL_GW ** -0.5),
        "pool_b_grp": nrm(ks[13], (N_B, D_POOL), 0.01),
        "pool_scale": 1.0 + nrm(ks[14], (N_B, D_POOL), 0.1),
        "pool_w_out": nrm(ks[15], (N_B, D_POOL, D_MODEL), BETA * D_POOL ** -0.5),
        "ln_mix_g": 1.0 + nrm(ks[16], (DEPTH, D_MODEL), 0.05),
        "ln_mix_b": nrm(ks[17], (DEPTH, D_MODEL), 0.01),
        "mlp_w1": nrm(ks[18], (DEPTH, D_MODEL, D_FF), D_MODEL ** -0.5),
        "mlp_w2": nrm(ks[19], (DEPTH, D_FF, D_MODEL), BETA * D_FF ** -0.5),
        "ln_mlp_g": 1.0 + nrm(ks[20], (DEPTH, D_MODEL), 0.05),
        "ln_mlp_b": nrm(ks[21], (DEPTH, D_MODEL), 0.01),
        "ple_w": nrm(ks[22], (DEPTH, PLE_DIM, D_MODEL), PLE_DIM ** -0.5),
        "ple_gate_w": nrm(ks[23], (DEPTH, D_MODEL, D_MODEL), D_MODEL ** -0.5),
        "ple_gate_b": nrm(ks[24], (DEPTH, D_MODEL), 0.01),
    }


def reference(x, p, lru_w_in, lru_conv_w, lru_conv_b, lru_wa, lru_ba, lru_wx, lru_bx,
              lru_lambda, lru_w_out, pool_w_in, pool_w_grp, pool_b_grp, pool_scale,
              pool_w_out, ln_mix_g, ln_mix_b, mlp_w1, mlp_w2, ln_mlp_g, ln_mlp_b,
              ple_w, ple_gate_w, ple_gate_b):
    for i in range(DEPTH):
        slot = i // N_MIXERS
        if i % N_MIXERS == 0:
            m = recurrent_mixer(x, lru_w_in[slot], lru_conv_w[slot], lru_conv_b[slot],
                                lru_wa[slot], lru_ba[slot], lru_wx[slot], lru_bx[slot],
                                lru_lambda[slot], lru_w_out[slot])
        else:
            m = pooling_mixer(x, pool_w_in[slot], pool_w_grp[slot], pool_b_grp[slot],
                              pool_scale[slot], pool_w_out[slot])
        x = layer_norm(ALPHA * x + m, ln_mix_g[i], ln_mix_b[i])
        x = layer_norm(ALPHA * x + sq_relu_mlp(x, mlp_w1[i], mlp_w2[i]), ln_mlp_g[i], ln_mlp_b[i])
        gate = jax.nn.sigmoid(x @ ple_gate_w[i] + ple_gate_b[i])
        x = x + (p[i] @ ple_w[i]) * gate
    return x
```

```python
import numpy as np
from contextlib import ExitStack
import concourse.bass as bass
import concourse.mybir as mybir
from concourse.bass_utils import run_bass_kernel_spmd

F32 = mybir.dt.float32
BF16 = mybir.dt.bfloat16
AF = mybir.ActivationFunctionType
ALU = mybir.AluOpType

D = 1024
S = 4096
DEPTH = 4
DR = 1536
NH = 12
DFF = 4096
PLE = 256
KC = D // 128
T = 1024
TB = 512
NB = T // TB
NST = S // T
ALPHA = float((2 * DEPTH) ** 0.25)
LN_EPS = 1e-5
GC1 = float(np.sqrt(2.0 / np.pi))
GC2 = float(np.sqrt(2.0 / np.pi) * 0.044715)
NSLOT = 5
SLOT_ELEMS = 4096
POOL_W = (2, 4, 8, 16)
HIST = 16


def _const_layout():
    off = {}
    n = 0
    for i in range(DEPTH):
        if i % 2 == 0:
            for nm, w in (("convw", 4 * NH), ("convb", NH), ("ba", NH), ("bx", NH), ("lam", NH)):
                off[(i, nm)] = (n, w); n += w
        else:
            for nm, w in (("bgrp", KC), ("pscale", KC)):
                off[(i, nm)] = (n, w); n += w
        for nm in ("g1", "b1", "g2", "b2", "gb"):
            off[(i, nm)] = (n, KC); n += KC
    return off, n


CST_OFF, NCST = _const_layout()


def _derived_layout():
    off = {}
    n = 0
    for i in range(DEPTH):
        if i % 2 == 0:
            for nm in ("hba", "hbx", "cneg", "hcn"):
                off[(i, nm)] = (n, NH); n += NH
        else:
            off[(i, "bsc")] = (n, KC); n += KC
        off[(i, "hgb")] = (n, KC); n += KC
    return off, n


DRV_OFF, NDRV = _derived_layout()

W_KINDS = {
    "A": {"win": (NH, KC * 256), "wout": (4, NH * 256)},
    "B": {"win": (2, KC * 512), "wgrp": (1, 4 * 2 * 256), "wout": (2, KC * 512)},
    "C": {"w1": (8, KC * 512), "w2": (8, 32 * 128), "gatew": (2, KC * 512)},
}
GATES_ELEMS = 2 * NH * 128
PLEW_ELEMS = 2 * 1024


def _layer_kinds(i):
    d = dict(W_KINDS["A" if i % 2 == 0 else "B"])
    d.update(W_KINDS["C"])
    return d


def _piece_order(layers, nst):
    order = []
    for st in range(nst):
        for i in layers:
            if i % 2 == 0:
                for h in range(NH):
                    order.append((i, "win", h))
                for q in range(4):
                    order.append((i, "wout", q))
            else:
                for q in range(2):
                    order.append((i, "win", q))
                order.append((i, "wgrp", 0))
                for q in range(2):
                    order.append((i, "wout", q))
            for q in range(8):
                order.append((i, "w1", q))
            for q in range(8):
                order.append((i, "w2", q))
            for q in range(2):
                order.append((i, "gatew", q))
    return order


class Eng:
    def __init__(self, name, h, sem, kind):
        self.name, self.h, self.sem, self.kind = name, h, sem, kind
        self.cnt = 0
        self.seen = {}


class Stream:
    def __init__(self, name, sem):
        self.name, self.sem = name, sem
        self.cnt = 0
        self.kind = "dma"


class Buf:
    __slots__ = ("w", "r", "name", "excl")

    def __init__(self, name="", excl=False):
        self.w = None
        self.r = {}
        self.name = name
        self.excl = excl


class Prog:
    def __init__(self, nc, es):
        self.nc = nc
        self.es = es
        self._nsem = 0
        self.PE = self._eng("pe", nc.tensor, "pe")
        self.ACT = self._eng("act", nc.scalar, "act")
        self.DVE = self._eng("dve", nc.vector, "dve")
        self.POOL = self._eng("pool", nc.gpsimd, "pool")
        self.SP = self._eng("sp", nc.sync, "sp")

    def sem(self, name):
        self._nsem += 1
        return self.es.enter_context(self.nc.semaphore(f"{name}_{self._nsem}"))

    def _eng(self, name, h, kind):
        return Eng(name, h, self.sem("e_" + name), kind)

    def stream(self, name):
        return Stream(name, self.sem("s_" + name))

    def _sync(self, E, reads, writes):
        deps = {}

        def need(src, c):
            if deps.get(src, 0) < c:
                deps[src] = c

        for b in reads:
            if b.w is not None:
                need(*b.w)
            if b.excl:
                for src, c in b.r.items():
                    if src is not E:
                        need(src, c)
        for b in writes:
            if b.w is not None:
                need(*b.w)
            for src, c in b.r.items():
                need(src, c)
        todo = []
        for src, c in deps.items():
            if src is E:
                if E.kind == "pe":
                    continue
            if E.seen.get(src, 0) >= c:
                continue
            todo.append((src, c))
            E.seen[src] = c
        for src, c in todo[:-1]:
            E.h.wait_ge(src.sem, c)
        return todo[-1] if todo else None

    @staticmethod
    def _record(reads, writes, ev):
        src, c = ev
        for b in reads:
            if b.r.get(src, 0) < c:
                b.r[src] = c
        for b in writes:
            b.w = ev
            b.r = {}

    def op(self, E, fn, reads=(), writes=(), inc=True):
        w = self._sync(E, reads, writes)
        ins = fn(E.h)
        if w is not None:
            ins._wait_ge(w[0].sem, w[1])
        if inc:
            E.cnt += 1
            ins.then_inc(E.sem, 1)
            ev = (E, E.cnt)
        else:
            ev = (E, E.cnt + 1)
        self._record(reads, writes, ev)

    def dma(self, E, stream, out, in_, reads=(), writes=()):
        w = self._sync(E, reads, writes)
        ins = E.h.dma_start(out=out, in_=in_)
        if w is not None:
            ins._wait_ge(w[0].sem, w[1])
        ins.then_inc(stream.sem, 16)
        stream.cnt += 16
        self._record(reads, writes, (stream, stream.cnt))

    def barrier(self, engs):
        for E in engs:
            for F in engs:
                if F is E:
                    continue
                if F.cnt > E.seen.get(F, 0):
                    E.h.wait_ge(F.sem, F.cnt)
                    E.seen[F] = F.cnt

    def wait_stream(self, E, stream):
        if stream.cnt > E.seen.get(stream, 0):
            E.h.wait_ge(stream.sem, stream.cnt)
            E.seen[stream] = stream.cnt


def build(layers=(0, 1, 2, 3), nst=NST):
    nc = bass.Bass("TRN2", target_bir_lowering=False)
    xT_d = nc.dram_tensor("xT", [D, S], F32, kind="ExternalInput").ap()
    pT_d = nc.dram_tensor("pT", [DEPTH, PLE, S], F32, kind="ExternalInput").ap()
    cst_d = nc.dram_tensor("cst", [128, NCST], F32, kind="ExternalInput").ap()
    outT_d = nc.dram_tensor("outT", [D, S], F32, kind="ExternalOutput").ap()
    W_d = {}
    for i in layers:
        for kind, (npc, fe) in _layer_kinds(i).items():
            W_d[(i, kind)] = nc.dram_tensor(f"w{i}_{kind}", [npc, 128, fe], F32, kind="ExternalInput").ap()

    G_d, PW_d = {}, {}
    for i in layers:
        if i % 2 == 0:
            G_d[i] = nc.dram_tensor(f"w{i}_gates", [128, GATES_ELEMS], F32, kind="ExternalInput").ap()
        PW_d[i] = nc.dram_tensor(f"w{i}_plew", [128, PLEW_ELEMS], F32, kind="ExternalInput").ap()

    with ExitStack() as es:
        P = Prog(nc, es)
        PE, ACT, DVE, POOL, SP = P.PE, P.ACT, P.DVE, P.POOL, P.SP

        def sb(name, shape, dt):
            return es.enter_context(nc.sbuf_tensor(name, shape, dt))

        xf = sb("xf", [128, KC, T], F32)
        xbA = sb("xbA", [128, KC, T], BF16)
        xbB = sb("xbB", [128, KC, T], BF16)
        ring = [sb(f"ring{k}", [128, SLOT_ELEMS], BF16) for k in range(NSLOT)]
        SCR_F = 16448
        scr = sb("scr", [128, SCR_F], F32)
        scr_b = scr[:].bitcast(BF16)
        NTMP = 6
        tmps = [sb(f"tmp{k}", [128, TB], F32) for k in range(NTMP)]
        stat = [sb(f"stat{k}", [128, TB], F32) for k in range(3)]
        pb = sb("pb", [128, 2, T], BF16)
        gates_t = sb("gates_t", [128, GATES_ELEMS], BF16)
        plew_t = sb("plew_t", [128, PLEW_ELEMS], BF16)
        cs = sb("cs", [128, NCST], F32)
        dc = sb("dc", [128, NDRV], F32)
        kc_t = sb("kc", [128, 8], F32)
        ones = sb("ones", [128, 128], BF16)
        lru_state = sb("lru_state", [128, 2, NH], F32)
        conv_hist = sb("conv_hist", [128, 2, NH, 4], F32)
        pool_hist = sb("pool_hist", [128, 2, KC, HIST], F32)
        invc = sb("invc", [128, 4, HIST], F32)
        psum = [es.enter_context(nc.psum_tensor(f"ps{k}", [128, TB], F32)) for k in range(8)]

        B_xf = [[Buf(f"xf{c}_{b}") for b in range(NB)] for c in range(KC)]
        B_xbA = [[Buf() for b in range(NB)] for c in range(KC)]
        B_xbB = [[Buf() for b in range(NB)] for c in range(KC)]
        B_ring = [Buf() for _ in range(NSLOT)]
        B_tmp = [Buf() for _ in range(NTMP)]
        B_stat = [Buf() for _ in range(3)]
        B_ps = [Buf(excl=True) for _ in range(8)]
        B_pb = Buf()
        B_gates = Buf()
        B_plew = Buf()
        S_g = P.stream("g")
        S_pw = P.stream("pw")
        g_sched = [i for _st in range(nst) for i in layers if i % 2 == 0]
        pw_sched = [i for _st in range(nst) for i in layers]
        pre_state = {"g": 0, "pw": 0}

        def issue_gates():
            k = pre_state["g"]
            if k < len(g_sched):
                P.dma(POOL, S_g, gates_t[:], G_d[g_sched[k]], reads=(), writes=[B_gates])
                pre_state["g"] = k + 1

        def issue_plew():
            k = pre_state["pw"]
            if k < len(pw_sched):
                P.dma(POOL, S_pw, plew_t[:], PW_d[pw_sched[k]], reads=(), writes=[B_plew])
                pre_state["pw"] = k + 1
        B_state = Buf()
        S_ring = [P.stream(f"ring{k}") for k in range(NSLOT)]
        S_x = [P.stream(f"x{c}") for c in range(KC)]
        S_out = [P.stream(f"out{c}") for c in range(KC)]
        S_p = P.stream("p")
        S_xb = P.stream("xb")
        S_c = P.stream("c")

        def xslice(t, c, b):
            return t[:, c, b * TB:(b + 1) * TB]

        rot = {"ps": 0, "tmp": 0, "pool": list(range(8))}

        def next_ps():
            pool = rot["pool"]
            k = pool[rot["ps"] % len(pool)]; rot["ps"] += 1
            return psum[k][:], B_ps[k]

        def next_tmp():
            k = rot["tmp"]; rot["tmp"] = (k + 1) % NTMP
            return tmps[k], B_tmp[k]

        plist = _piece_order(layers, nst)
        ring_state = {"issued": 0, "pos": 0}

        def ring_issue_upto(k):
            while ring_state["issued"] <= min(k, len(plist) - 1):
                j = ring_state["issued"]
                li, kind, pi = plist[j]
                fe = _layer_kinds(li)[kind][1]
                s = j % NSLOT
                P.dma(POOL, S_ring[s], ring[s][:, 0:fe], W_d[(li, kind)][pi], reads=(), writes=[B_ring[s]])
                ring_state["issued"] = j + 1

        def next_piece(li, kind, pi):
            k = ring_state["pos"]
            assert plist[k] == (li, kind, pi), (plist[k], li, kind, pi)
            ring_issue_upto(k + NSLOT - 2)
            ring_state["pos"] = k + 1
            s = k % NSLOT
            return ring[s], B_ring[s]

        def qb_order(nq, head=0, tail=0):
            order = []
            order += [(q, b) for b in range(NB) for q in range(head)]
            order += [(q, b) for q in range(head, nq - tail) for b in range(NB)]
            order += [(q, b) for b in range(NB) for q in range(nq - tail, nq)]
            return order

        def run_pieces(i, kind, order, fn):
            handles = {}
            for q, b in order:
                if q not in handles:
                    handles[q] = next_piece(i, kind, q)
                fn(q, b, *handles[q])

        def act(out, in_, func, reads, writes, bias=None, scale=1.0):
            if bias is None:
                P.op(ACT, lambda h: h.activation(out=out, in_=in_, func=func, scale=scale), reads, writes)
            else:
                P.op(ACT, lambda h: h.activation(out=out, in_=in_, func=func, bias=bias, scale=scale), reads, writes)

        def tt(E, out, a, b, op, reads, writes):
            P.op(E, lambda h: h.tensor_tensor(out=out, in0=a, in1=b, op=op), reads, writes)

        def ts(E, out, in0, s1, s2, op0, op1, reads, writes):
            if op1 is None:
                P.op(E, lambda h: h.tensor_scalar(out=out, in0=in0, scalar1=s1, scalar2=None, op0=op0), reads, writes)
            else:
                P.op(E, lambda h: h.tensor_scalar(out=out, in0=in0, scalar1=s1, scalar2=s2, op0=op0, op1=op1), reads, writes)

        def stt(out, in0, scalar, in1, op0, op1, reads, writes):
            P.op(DVE, lambda h: h.scalar_tensor_tensor(out=out, in0=in0, scalar=scalar, in1=in1, op0=op0, op1=op1),
                 reads, writes)

        def mm_group(ps_ap, ps_buf, terms):
            n = len(terms)
            for idx, (l, r, rb) in enumerate(terms):
                P.op(PE, lambda h, l=l, r=r, idx=idx: h.matmul(ps_ap, lhsT=l, rhs=r, start=(idx == 0), stop=(idx == n - 1)),
                     reads=rb, writes=[ps_buf], inc=(idx == n - 1))

        def cst(i, nm):
            o, w = CST_OFF[(i, nm)]
            return cs[:, o:o + w]

        def drv(i, nm):
            o, w = DRV_OFF[(i, nm)]
            return dc[:, o:o + w]

        KONE, KSIX, KEPS, KZERO = 0, 1, 2, 3

        P.dma(SP, S_c, cs[:], cst_d, reads=(), writes=())
        ring_issue_upto(NSLOT - 3)
        issue_gates()
        issue_plew()
        DVE.h.memset(kc_t[:, KONE:KONE + 1], 1.0)
        DVE.h.memset(kc_t[:, KSIX:KSIX + 1], 1.0 / 16.0)
        DVE.h.memset(kc_t[:, KEPS:KEPS + 1], LN_EPS)
        DVE.h.memset(kc_t[:, KZERO:KZERO + 1], 0.0)
        DVE.h.memset(ones[:], 1.0 / D)
        DVE.h.memset(lru_state[:], 0.0)
        DVE.h.memset(conv_hist[:], 0.0)
        DVE.h.memset(pool_hist[:], 0.0)
        for g, w in enumerate(POOL_W):
            for t in range(HIST):
                DVE.h.memset(invc[:, g, t:t + 1], 1.0 / min(t + 1, w))
        B_init = Buf()
        P.op(DVE, lambda h: h.memset(dc[:], 0.0), reads=(), writes=[B_init])
        P.wait_stream(DVE, S_c)
        P.wait_stream(ACT, S_c)
        for i in layers:
            if i % 2 == 0:
                ts(DVE, drv(i, "hba"), cst(i, "ba"), 0.5, None, ALU.mult, None, [B_init], [B_init])
                ts(DVE, drv(i, "hbx"), cst(i, "bx"), 0.5, None, ALU.mult, None, [B_init], [B_init])
                act(drv(i, "cneg"), cst(i, "lam"), AF.Exp, [B_init], [B_init], scale=-1.0)
                act(drv(i, "cneg"), drv(i, "cneg"), AF.Ln, [B_init], [B_init], bias=kc_t[:, KONE:KONE + 1])
                ts(DVE, drv(i, "hcn"), drv(i, "cneg"), -4.0, None, ALU.mult, None, [B_init], [B_init])
                ts(DVE, drv(i, "cneg"), drv(i, "cneg"), -8.0, None, ALU.mult, None, [B_init], [B_init])
            else:
                tt(DVE, drv(i, "bsc"), cst(i, "bgrp"), cst(i, "pscale"), ALU.mult, [B_init], [B_init])
            ts(DVE, drv(i, "hgb"), cst(i, "gb"), 0.5, None, ALU.mult, None, [B_init], [B_init])
        P.barrier([PE, ACT, DVE])

        STAT_BANKS = {0: (4, 5), 1: (6, 7)}
        ln = {"cnt": [0] * NB, "pend": [], "done": 0}

        def ln_begin():
            rot["pool"] = [0, 1, 2, 3]
            ln["cnt"] = [0] * NB
            ln["done"] = 0

        def emit_stat(b, t1b, Bt1, t2b, Bt2, first, last):
            km, kq = STAT_BANKS[b]
            P.op(PE, lambda h: h.matmul(psum[km][:], lhsT=ones[:], rhs=t1b, start=first, stop=last),
                 reads=[Bt1], writes=[B_ps[km]], inc=False)
            P.op(PE, lambda h: h.matmul(psum[kq][:], lhsT=ones[:], rhs=t2b, start=first, stop=last),
                 reads=[Bt2], writes=[B_ps[kq]], inc=True)

        def ln_finish(i, which, b):
            g_ap, b_ap = cst(i, "g" + which), cst(i, "b" + which)
            km, kq = STAT_BANKS[b]
            ps_m, Bm, ps_q, Bq = psum[km][:], B_ps[km], psum[kq][:], B_ps[kq]
            mean, var, rstd = stat[0][:], stat[1][:], stat[2][:]
            P.op(DVE, lambda h: h.tensor_copy(out=mean, in_=ps_m), [Bm], [B_stat[0]])
            act(var, ps_m, AF.Square, [Bm], [B_stat[1]])
            tt(DVE, var, ps_q, var, ALU.subtract, [Bq, B_stat[1]], [B_stat[1]])
            act(var, var, AF.Ln, [B_stat[1]], [B_stat[1]], bias=kc_t[:, KEPS:KEPS + 1])
            act(rstd, var, AF.Exp, [B_stat[1]], [B_stat[2]], scale=-0.5)
            for c in range(KC):
                t1, Bt1 = next_tmp()
                tt(DVE, t1[:], xslice(xf, c, b), mean, ALU.subtract, [B_xf[c][b], B_stat[0]], [Bt1])
                tt(DVE, t1[:], t1[:], rstd, ALU.mult, [Bt1, B_stat[2]], [Bt1])
                act(xslice(xbA, c, b), t1[:], AF.Identity, [Bt1], [B_xbA[c][b]],
                    bias=b_ap[:, c:c + 1], scale=g_ap[:, c:c + 1])
                ts(POOL, xslice(xf, c, b), t1[:], g_ap[:, c:c + 1], b_ap[:, c:c + 1], ALU.mult, ALU.add,
                   [Bt1], [B_xf[c][b]])
            ln["done"] += 1
            if ln["done"] == NB:
                rot["pool"] = list(range(8))

        def residual_epilogue(ps, Bp, m, b, i, which):
            stt(xslice(xf, m, b), xslice(xf, m, b), ALPHA, ps, ALU.mult, ALU.add, [B_xf[m][b], Bp], [B_xf[m][b]])
            t1, Bt1 = next_tmp()
            t2, Bt2 = next_tmp()
            t1b = t1[:].bitcast(BF16)[:, 0:TB]
            t2b = t2[:].bitcast(BF16)[:, 0:TB]
            act(t1b, xslice(xf, m, b), AF.Copy, [B_xf[m][b]], [Bt1])
            act(t2b, xslice(xf, m, b), AF.Square, [B_xf[m][b]], [Bt2])
            first = ln["cnt"][b] == 0
            last = ln["cnt"][b] == KC - 1
            ln["cnt"][b] += 1
            ln["pend"].append((b, t1b, Bt1, t2b, Bt2, first, last))
            while len(ln["pend"]) > (0 if last else 1):
                emit_stat(*ln["pend"].pop(0))
            if last:
                ln_finish(i, which, b)

        def lru_mixer(i, st, xb_in, B_xb_in):
            slot = i // 2
            o = 0

            def carve_f(n):
                nonlocal o
                a = scr[:, o:o + n]; o += n
                return a
            S1, S2 = 3, 2
            upre = [carve_f(TB + 4) for _ in range(S1)]
            yz = [carve_f(TB) for _ in range(S1)]
            ycp = [carve_f(TB) for _ in range(S1)]
            u_s = [carve_f(TB) for _ in range(S2)]
            tr_s = [carve_f(TB) for _ in range(S2)]
            ti_s = [carve_f(TB) for _ in range(S2)]
            a_s = [carve_f(TB) for _ in range(S2)]
            e2_s = [carve_f(TB) for _ in range(S2)]
            ub_s = []
            for _ in range(S2):
                ub_s.append(scr_b[:, 2 * o:2 * o + TB]); o += TB // 2
            g_t = scr_b[:, 2 * o:2 * o + NH * T]; o += NH * T // 2
            assert o <= SCR_F, o
            Bup = [Buf() for _ in range(S1)]; Byz = [Buf() for _ in range(S1)]; Byc = [Buf() for _ in range(S1)]
            Bu = [Buf() for _ in range(S2)]; Btr = [Buf() for _ in range(S2)]; Bti = [Buf() for _ in range(S2)]
            Ba = [Buf() for _ in range(S2)]; Be2 = [Buf() for _ in range(S2)]; Bub = [Buf() for _ in range(S2)]
            tr_s.append(tmps[0][:]); Btr.append(B_tmp[0])
            ti_s.append(tmps[1][:]); Bti.append(B_tmp[1])
            a_s.append(tmps[2][:]); Ba.append(B_tmp[2])
            e2_s.append(tmps[3][:]); Be2.append(B_tmp[3])
            yz.append(tmps[4][:]); Byz.append(B_tmp[4])
            ycp.append(tmps[5][:]); Byc.append(B_tmp[5])
            SY, SG = 4, 3
            Bg = [[Buf() for _ in range(NB)] for _ in range(NH)]
            cw = cst(i, "convw")
            cb = cst(i, "convb")
            hba, hbx, cneg, hcn = drv(i, "hba"), drv(i, "hbx"), drv(i, "cneg"), drv(i, "hcn")
            gslot, Bgs = gates_t, B_gates
            wcur = {}
            NU = NH * NB

            def stage1(k):
                h, b = divmod(k, NB)
                s1, sy = k % S1, k % SY
                if b == 0:
                    wcur["w"] = next_piece(i, "win", h)
                wslot, Bw = wcur["w"]
                ps_u, Bpu = next_ps()
                mm_group(ps_u, Bpu, [(wslot[:, kc * 256:kc * 256 + 128], xslice(xb_in, kc, b), [Bw, B_xb_in[kc][b]])
                                     for kc in range(KC)])
                ps_y, Bpy = next_ps()
                mm_group(ps_y, Bpy, [(wslot[:, kc * 256 + 128:kc * 256 + 256], xslice(xb_in, kc, b), [Bw, B_xb_in[kc][b]])
                                     for kc in range(KC)])
                act(upre[s1][:, 4:4 + TB], ps_u, AF.Copy, [Bpu], [Bup[s1]])
                act(yz[sy], ps_y, AF.Square, [Bpy], [Byz[sy]], scale=float(np.sqrt(GC2)))
                act(ycp[sy], ps_y, AF.Copy, [Bpy], [Byc[sy]])

            def conv(k):
                h, b = divmod(k, NB)
                s1, s2, sy, sg = k % S1, k % S2, k % SY, k % SG
                up, u_t = upre[s1], u_s[s2]
                if b == 0:
                    P.op(DVE, lambda hh: hh.tensor_copy(out=up[:, 1:4], in_=conv_hist[:, slot, h, 1:4]),
                         [B_state], [Bup[s1]])
                else:
                    sp = (k - 1) % S1
                    P.op(DVE, lambda hh: hh.tensor_copy(out=up[:, 1:4], in_=upre[sp][:, TB + 1:TB + 4]),
                         [Bup[sp]], [Bup[s1]])
                ts(DVE, u_t, up[:, 4:4 + TB], cw[:, 3 * NH + h:3 * NH + h + 1], cb[:, h:h + 1], ALU.mult, ALU.add,
                   [Bup[s1]], [Bu[s2]])
                for kk in (2, 1, 0):
                    stt(u_t, up[:, 1 + kk:1 + kk + TB], cw[:, kk * NH + h:kk * NH + h + 1], u_t, ALU.mult, ALU.add,
                        [Bup[s1], Bu[s2]], [Bu[s2]])
                if b == NB - 1:
                    P.op(DVE, lambda hh: hh.tensor_copy(out=conv_hist[:, slot, h, 1:4], in_=up[:, TB + 1:TB + 4]),
                         [Bup[s1]], [B_state])

            def ubg(k):
                h, b = divmod(k, NB)
                s2, sg = k % S2, k % SG
                P.op(DVE, lambda hh: hh.tensor_copy(out=ub_s[s2], in_=u_s[s2]), [Bu[s2]], [Bub[s2]])
                ps_r, Bpr = next_ps()
                mm_group(ps_r, Bpr, [(gslot[:, h * 128:(h + 1) * 128], ub_s[s2], [Bgs, Bub[s2]])])
                ps_i, Bpi = next_ps()
                mm_group(ps_i, Bpi, [(gslot[:, (NH + h) * 128:(NH + h + 1) * 128], ub_s[s2], [Bgs, Bub[s2]])])
                return ps_r, Bpr, ps_i, Bpi

            def gate_act(k, ps_r, Bpr, ps_i, Bpi):
                h, b = divmod(k, NB)
                s2, sg = k % S2, k % SG
                act(ti_s[sg], ps_i, AF.Tanh, [Bpi], [Bti[sg]], bias=hbx[:, h:h + 1], scale=0.5)
                act(tr_s[sg], ps_r, AF.Tanh, [Bpr], [Btr[sg]], bias=hba[:, h:h + 1], scale=0.5)
                act(a_s[sg], tr_s[sg], AF.Exp, [Btr[sg]], [Ba[sg]], bias=hcn[:, h:h + 1], scale=hcn[:, h:h + 1])
                act(e2_s[sg], tr_s[sg], AF.Exp, [Btr[sg]], [Be2[sg]], bias=cneg[:, h:h + 1], scale=cneg[:, h:h + 1])

            def mid_dve(k):
                s1, s2, sy, sg = k % S1, k % S2, k % SY, k % SG
                stt(ti_s[sg], ti_s[sg], 1.0, u_s[s2], ALU.add, ALU.mult, [Bti[sg], Bu[s2]], [Bti[sg]])
                stt(yz[sy], yz[sy], GC1, ycp[sy], ALU.add, ALU.mult, [Byz[sy], Byc[sy]], [Byz[sy]])
                ts(DVE, e2_s[sg], e2_s[sg], 1.0, -1.0 / 16.0, ALU.min, ALU.mult, [Be2[sg]], [Be2[sg]])

            def tanh_g(k):
                sy = k % SY
                act(yz[sy], yz[sy], AF.Tanh, [Byz[sy]], [Byz[sy]])

            def sqrt_k(k):
                sg = k % SG
                act(e2_s[sg], e2_s[sg], AF.Sqrt, [Be2[sg]], [Be2[sg]], bias=kc_t[:, KSIX:KSIX + 1], scale=1.0)

            def h2_dve(k):
                h, b = divmod(k, NB)
                s1, s2, sy, sg = k % S1, k % S2, k % SY, k % SG
                tr_t, ti_t, a_t, e2_t = tr_s[sg], ti_s[sg], a_s[sg], e2_s[sg]
                if st == 0 and b == 0:
                    P.op(DVE, lambda hh: hh.memset(e2_t[:, 0:1], 0.25), [], [Be2[sg]])
                tt(DVE, ti_t, ti_t, e2_t, ALU.mult, [Bti[sg], Be2[sg]], [Bti[sg]])
                P.op(DVE, lambda hh: hh.tensor_tensor_scan(out=tr_t, data0=a_t, data1=ti_t,
                                                           initial=lru_state[:, slot, h:h + 1],
                                                           op0=ALU.mult, op1=ALU.add),
                     [Ba[sg], Bti[sg], B_state], [Btr[sg]])
                P.op(DVE, lambda hh: hh.tensor_copy(out=lru_state[:, slot, h:h + 1], in_=tr_t[:, TB - 1:TB]),
                     [Btr[sg]], [B_state])
                stt(yz[sy], yz[sy], 1.0, ycp[sy], ALU.add, ALU.mult, [Byz[sy], Byc[sy]], [Byz[sy]])
                tt(DVE, g_t[:, h * T + b * TB:h * T + (b + 1) * TB], tr_t, yz[sy], ALU.mult,
                   [Btr[sg], Byz[sy]], [Bg[h][b]])

            stage1(0)
            pend = {}
            for t in range(NU + 3):
                cs = [t - 3, t - 2] if (0 <= t - 2 < NU and (t - 2) % 2 == 1) else []
                for k in cs:
                    tanh_g(k)
                for k in cs:
                    sqrt_k(k)
                if t < NU:
                    conv(t)
                    pend[t] = ubg(t)
                if 0 <= t - 1 < NU:
                    gate_act(t - 1, *pend.pop(t - 1))
                if 0 <= t - 3 < NU:
                    h2_dve(t - 3)
                if 0 <= t - 1 < NU:
                    mid_dve(t - 1)
                if t + 1 < NU:
                    stage1(t + 1)
            issue_gates()
            ln_begin()

            def wout_fn(q, b, wslot, Bw):
                for mm in range(2):
                    m = q * 2 + mm
                    ps, Bp = next_ps()
                    mm_group(ps, Bp, [(wslot[:, kc * 256 + mm * 128:kc * 256 + (mm + 1) * 128],
                                       g_t[:, kc * T + b * TB:kc * T + (b + 1) * TB], [Bw, Bg[kc][b]])
                                      for kc in range(NH)])
                    residual_epilogue(ps, Bp, m, b, i, "1")
            run_pieces(i, "wout", qb_order(4, tail=2), wout_fn)

        def pool_mixer(i, st, xb_in, B_xb_in):
            slot = i // 2
            o = 0
            def carve_f(n):
                nonlocal o
                a = scr[:, o:o + n]; o += n
                return a
            EXT = HIST + T
            uext = [carve_f(EXT) for _ in range(KC)]
            stmp = [carve_f(EXT) for _ in range(2)]
            zb = scr_b[:, 2 * o:2 * o + KC * T]; o += KC * T // 2
            assert o <= SCR_F, o
            z2b = scr_b[:, 0:KC * T]
            Bue = [Buf() for _ in range(KC)]
            Bst = [Buf(), Buf()]
            Bzb = [[Buf() for _ in range(NB)] for _ in range(KC)]
            bsc, psc = drv(i, "bsc"), cst(i, "pscale")
            for m in range(KC):
                P.op(DVE, lambda hh, m=m: hh.tensor_copy(out=uext[m][:, 0:HIST], in_=pool_hist[:, slot, m, :]),
                     [B_state], [Bue[m]])

            def win_fn(q, b, wslot, Bw):
                for mm in range(4):
                    m = q * 4 + mm
                    ps, Bp = next_ps()
                    mm_group(ps, Bp, [(wslot[:, kc * 512 + mm * 128:kc * 512 + (mm + 1) * 128], xslice(xb_in, kc, b),
                                       [Bw, B_xb_in[kc][b]]) for kc in range(KC)])
                    act(uext[m][:, HIST + b * TB:HIST + (b + 1) * TB], ps, AF.Copy, [Bp], [Bue[m]])

            def window(m):
                g = m // 2
                w = POOL_W[g]
                ue = uext[m]
                P.op(DVE, lambda hh: hh.tensor_copy(out=pool_hist[:, slot, m, :], in_=ue[:, T:T + HIST]),
                     [Bue[m]], [B_state])
                cur, Bcur = ue, Bue[m]
                sh = 1
                k = 0
                while sh < w:
                    dst, Bd = stmp[k % 2], Bst[k % 2]
                    tt(DVE, dst[:, sh:EXT], cur[:, sh:EXT], cur[:, 0:EXT - sh], ALU.add, [Bcur], [Bd])
                    cur, Bcur = dst, Bd
                    sh *= 2
                    k += 1
                if st == 0:
                    tt(DVE, cur[:, HIST:2 * HIST], cur[:, HIST:2 * HIST], invc[:, g, :], ALU.mult, [Bcur], [Bcur])
                    ts(DVE, cur[:, 2 * HIST:EXT], cur[:, 2 * HIST:EXT], 1.0 / w, None, ALU.mult, None, [Bcur], [Bcur])
                    for b in range(NB):
                        tt(DVE, zb[:, m * T + b * TB:m * T + (b + 1) * TB], cur[:, HIST + b * TB:HIST + (b + 1) * TB],
                           ue[:, HIST + b * TB:HIST + (b + 1) * TB], ALU.subtract, [Bcur, Bue[m]], [Bzb[m][b]])
                else:
                    for b in range(NB):
                        stt(zb[:, m * T + b * TB:m * T + (b + 1) * TB], cur[:, HIST + b * TB:HIST + (b + 1) * TB], 1.0 / w,
                            ue[:, HIST + b * TB:HIST + (b + 1) * TB], ALU.mult, ALU.subtract, [Bcur, Bue[m]], [Bzb[m][b]])

            Bz2 = [[Buf() for _ in range(NB)] for _ in range(KC)]

            def grp(g, wslot, Bw):
                for b in range(NB):
                    for mm in range(2):
                        m = 2 * g + mm
                        ps, Bp = next_ps()
                        mm_group(ps, Bp, [(wslot[:, (g * 2 + kk) * 256 + mm * 128:(g * 2 + kk) * 256 + (mm + 1) * 128],
                                           zb[:, (2 * g + kk) * T + b * TB:(2 * g + kk) * T + (b + 1) * TB],
                                           [Bw, Bzb[2 * g + kk][b]]) for kk in range(2)])
                        act(z2b[:, m * T + b * TB:m * T + (b + 1) * TB], ps, AF.Identity, [Bp], [Bz2[m][b]] + Bue[0:4],
                            bias=bsc[:, m:m + 1], scale=psc[:, m:m + 1])

            run_pieces(i, "win", qb_order(2, head=2), win_fn)
            for m in range(4):
                window(m)
            wslot_g, Bw_g = next_piece(i, "wgrp", 0)
            for m in range(4, 8):
                window(m)
            for g in range(4):
                grp(g, wslot_g, Bw_g)
            ln_begin()

            def wout_fn(q, b, wslot, Bw):
                for mm in range(4):
                    m = q * 4 + mm
                    ps, Bp = next_ps()
                    mm_group(ps, Bp, [(wslot[:, kc * 512 + mm * 128:kc * 512 + (mm + 1) * 128],
                                       z2b[:, kc * T + b * TB:kc * T + (b + 1) * TB], [Bw, Bz2[kc][b]])
                                      for kc in range(KC)])
                    residual_epilogue(ps, Bp, m, b, i, "1")
            run_pieces(i, "wout", qb_order(2, tail=2), wout_fn)

        def mlp(i):
            NF = DFF // 128
            hb = scr_b[:, 0:NF * T]
            Bh = [[Buf() for _ in range(NB)] for _ in range(NF)]
            def w1_fn(q, b, wslot, Bw):
                for mm in range(4):
                    j = q * 4 + mm
                    ps, Bp = next_ps()
                    mm_group(ps, Bp, [(wslot[:, kc * 512 + mm * 128:kc * 512 + (mm + 1) * 128], xslice(xbA, kc, b),
                                       [Bw, B_xbA[kc][b]]) for kc in range(KC)])
                    t1, Bt1 = next_tmp()
                    rb = t1[:].bitcast(BF16)[:, 0:TB]
                    act(rb, ps, AF.Relu, [Bp], [Bt1])
                    tt(DVE, hb[:, j * T + b * TB:j * T + (b + 1) * TB], rb, rb, ALU.mult, [Bt1], [Bh[j][b]])
            run_pieces(i, "w1", qb_order(8, head=2), w1_fn)
            ln_begin()

            def w2_fn(q, b, wslot, Bw):
                ps, Bp = next_ps()
                mm_group(ps, Bp, [(wslot[:, j * 128:(j + 1) * 128], hb[:, j * T + b * TB:j * T + (b + 1) * TB],
                                   [Bw, Bh[j][b]]) for j in range(NF)])
                residual_epilogue(ps, Bp, q, b, i, "2")
            run_pieces(i, "w2", qb_order(8, tail=2), w2_fn)

        def ple(i, st):
            hgb = drv(i, "hgb")
            pslot, Bpw = plew_t, B_plew
            def gate_fn(q, b, wslot, Bw):
                if True:
                    for mm in range(4):
                        m = q * 4 + mm
                        ps_g, Bpg = next_ps()
                        mm_group(ps_g, Bpg, [(wslot[:, kc * 512 + mm * 128:kc * 512 + (mm + 1) * 128], xslice(xbA, kc, b),
                                              [Bw, B_xbA[kc][b]]) for kc in range(KC)])
                        ps_p, Bpp = next_ps()
                        mm_group(ps_p, Bpp, [(pslot[:, kk * 1024 + m * 128:kk * 1024 + (m + 1) * 128],
                                              pb[:, kk, b * TB:(b + 1) * TB], [Bpw, B_pb]) for kk in range(2)])
                        t1, Bt1 = next_tmp()
                        act(t1[:], ps_g, AF.Tanh, [Bpg], [Bt1], bias=hgb[:, m:m + 1], scale=0.5)
                        stt(t1[:], t1[:], 1.0, ps_p, ALU.add, ALU.mult, [Bt1, Bpp], [Bt1])
                        stt(xslice(xf, m, b), t1[:], 0.5, xslice(xf, m, b), ALU.mult, ALU.add, [Bt1, B_xf[m][b]], [B_xf[m][b]])
                        if i != layers[-1]:
                            act(xslice(xbB, m, b), xslice(xf, m, b), AF.Copy, [B_xf[m][b]], [B_xbB[m][b]])
            run_pieces(i, "gatew", qb_order(2, head=2), gate_fn)
            issue_plew()

        all_xf = [B_xf[c][b] for c in range(KC) for b in range(NB)]
        for st in range(nst):
            tok = slice(st * T, (st + 1) * T)
            for c in range(KC):
                P.dma(SP, S_x[c], xf[:, c, :], xT_d[c * 128:(c + 1) * 128, tok], reads=(), writes=B_xf[c])
            if st == 0:
                for c in range(KC):
                    for b in range(NB):
                        act(xslice(xbB, c, b), xslice(xf, c, b), AF.Copy, [B_xf[c][b]], [B_xbB[c][b]])
            for i in layers:
                P.dma(POOL, S_p, pb[:], pT_d[i][:, tok].rearrange("(k p) t -> p k t", p=128), reads=(), writes=[B_pb])
                if i % 2 == 0:
                    lru_mixer(i, st, xbB, B_xbB)
                else:
                    pool_mixer(i, st, xbB, B_xbB)
                if i == layers[-1] and st + 1 < nst:
                    ntok = slice((st + 1) * T, (st + 2) * T)
                    P.dma(POOL, S_xb, xbB[:], xT_d[:, ntok].rearrange("(c p) t -> p c t", p=128), reads=(),
                          writes=[B_xbB[c][b] for c in range(KC) for b in range(NB)])
                mlp(i)
                ple(i, st)
            for c in range(KC):
                P.dma(SP, S_out[c], outT_d[c * 128:(c + 1) * 128, tok], xf[:, c, :], reads=B_xf[c], writes=())
        for c in range(KC):
            P.wait_stream(SP, S_out[c])
        assert ring_state["pos"] == len(plist)
    return nc


def _vec(v, n):
    return np.ascontiguousarray(np.asarray(v, np.float32).reshape(n, 128).T)


def _pieces(Wm, cw):
    K, N = Wm.shape
    kc, npc = K // 128, N // cw
    return np.ascontiguousarray(Wm.reshape(kc, 128, npc, cw).transpose(2, 1, 0, 3)).reshape(npc, 128, kc * cw)


def prep_shared(inp, layers=(0, 1, 2, 3)):
    shared = {}
    cstv = np.zeros((128, NCST), np.float32)

    def put(i, nm, arr):
        o, w = CST_OFF[(i, nm)]
        cstv[:, o:o + w] = arr

    for i in layers:
        sl = i // 2
        if i % 2 == 0:
            cw = np.asarray(inp["lru_conv_w"][sl], np.float32)
            put(i, "convw", np.concatenate([_vec(cw[k], NH) for k in range(4)], axis=1))
            put(i, "convb", _vec(inp["lru_conv_b"][sl], NH))
            put(i, "ba", _vec(inp["lru_ba"][sl], NH))
            put(i, "bx", _vec(inp["lru_bx"][sl], NH))
            put(i, "lam", _vec(inp["lru_lambda"][sl], NH))
            win = np.asarray(inp["lru_w_in"][sl], np.float32)
            a = win.reshape(KC, 128, 2, NH, 128).transpose(3, 1, 0, 2, 4)
            shared[f"w{i}_win"] = np.ascontiguousarray(a).reshape(NH, 128, KC * 256)
            wa = np.asarray(inp["lru_wa"][sl], np.float32).transpose(1, 0, 2)
            wx = np.asarray(inp["lru_wx"][sl], np.float32).transpose(1, 0, 2)
            shared[f"w{i}_gates"] = np.ascontiguousarray(np.concatenate([wa, wx], axis=1)).reshape(128, 2 * NH * 128)
            shared[f"w{i}_wout"] = _pieces(np.asarray(inp["lru_w_out"][sl], np.float32), 256)
        else:
            put(i, "bgrp", _vec(inp["pool_b_grp"][sl], KC))
            put(i, "pscale", _vec(inp["pool_scale"][sl], KC))
            shared[f"w{i}_win"] = _pieces(np.asarray(inp["pool_w_in"][sl], np.float32), 512)
            wg = np.asarray(inp["pool_w_grp"][sl], np.float32)
            a = wg.reshape(4, 2, 128, 256).transpose(2, 0, 1, 3)
            shared[f"w{i}_wgrp"] = np.ascontiguousarray(a).reshape(1, 128, 4 * 2 * 256)
            shared[f"w{i}_wout"] = _pieces(np.asarray(inp["pool_w_out"][sl], np.float32), 512)
        put(i, "g1", _vec(inp["ln_mix_g"][i], KC))
        put(i, "b1", _vec(inp["ln_mix_b"][i], KC))
        put(i, "g2", _vec(inp["ln_mlp_g"][i], KC))
        put(i, "b2", _vec(inp["ln_mlp_b"][i], KC))
        put(i, "gb", _vec(inp["ple_gate_b"][i], KC))
        shared[f"w{i}_w1"] = _pieces(np.asarray(inp["mlp_w1"][i], np.float32), 512)
        shared[f"w{i}_w2"] = _pieces(np.asarray(inp["mlp_w2"][i], np.float32), 128)
        shared[f"w{i}_plew"] = _pieces(np.asarray(inp["ple_w"][i], np.float32), 1024)[0]
        shared[f"w{i}_gatew"] = _pieces(np.asarray(inp["ple_gate_w"][i], np.float32), 512)
    shared["cst"] = cstv
    return shared


def make_in_maps(inputs, cores, layers=(0, 1, 2, 3)):
    x = np.asarray(inputs["x"], np.float32)
    p = np.asarray(inputs["p"], np.float32)
    shared = prep_shared(inputs, layers)
    in_maps = []
    for c in cores:
        m = dict(shared)
        m["xT"] = np.ascontiguousarray(x[c].T)
        m["pT"] = np.ascontiguousarray(p[:, c].transpose(0, 2, 1))
        in_maps.append(m)
    return in_maps


def kernel(**inputs):
    x = np.asarray(inputs["x"], np.float32)
    p = np.asarray(inputs["p"], np.float32)
    n = x.shape[0]
    shared = prep_shared(inputs)
    in_maps = []
    for c in range(n):
        m = dict(shared)
        m["xT"] = np.ascontiguousarray(x[c].T)
        m["pT"] = np.ascontiguousarray(p[:, c].transpose(0, 2, 1))
        in_maps.append(m)
    nc = build()
    res = run_bass_kernel_spmd(nc, in_maps, core_ids=list(range(n)))
    out = np.stack([np.asarray(res.results[c]["outT"], np.float32).T for c in range(n)], axis=0)
    return np.ascontiguousarray(out)
```

```python
import numpy as np
from contextlib import ExitStack
import concourse.bass as bass
import concourse.mybir as mybir
from concourse.bass_utils import run_bass_kernel_spmd

F32 = mybir.dt.float32
BF16 = mybir.dt.bfloat16
AF = mybir.ActivationFunctionType
ALU = mybir.AluOpType

D = 1024
S = 4096
DEPTH = 4
DR = 1536
NH = 12
DFF = 4096
PLE = 256
KC = D // 128
T = 1024
TB = 512
NB = T // TB
NST = S // T
ALPHA = float((2 * DEPTH) ** 0.25)
LN_EPS = 1e-5
GC1 = float(np.sqrt(2.0 / np.pi))
GC2 = float(np.sqrt(2.0 / np.pi) * 0.044715)
NSLOT = 5
SLOT_ELEMS = 4096
POOL_W = (2, 4, 8, 16)
HIST = 16


def _const_layout():
    off = {}
    n = 0
    for i in range(DEPTH):
        if i % 2 == 0:
            for nm, w in (("convw", 4 * NH), ("convb", NH), ("ba", NH), ("bx", NH), ("lam", NH)):
                off[(i, nm)] = (n, w); n += w
        else:
            for nm, w in (("bgrp", KC), ("pscale", KC)):
                off[(i, nm)] = (n, w); n += w
        for nm in ("g1", "b1", "g2", "b2", "gb"):
            off[(i, nm)] = (n, KC); n += KC
    return off, n


CST_OFF, NCST = _const_layout()


def _derived_layout():
    off = {}
    n = 0
    for i in range(DEPTH):
        if i % 2 == 0:
            for nm in ("hba", "hbx", "cneg", "hcn"):
                off[(i, nm)] = (n, NH); n += NH
        else:
            off[(i, "bsc")] = (n, KC); n += KC
        off[(i, "hgb")] = (n, KC); n += KC
    return off, n


DRV_OFF, NDRV = _derived_layout()

W_KINDS = {
    "A": {"win": (NH, KC * 256), "wout": (4, NH * 256)},
    "B": {"win": (2, KC * 512), "wgrp": (1, 4 * 2 * 256), "wout": (2, KC * 512)},
    "C": {"w1": (8, KC * 512), "w2": (8, 32 * 128), "gatew": (2, KC * 512)},
}
GATES_ELEMS = 2 * NH * 128
PLEW_ELEMS = 2 * 1024


def _layer_kinds(i):
    d = dict(W_KINDS["A" if i % 2 == 0 else "B"])
    d.update(W_KINDS["C"])
    return d


def _piece_order(layers, nst):
    order = []
    for st in range(nst):
        for i in layers:
            if i % 2 == 0:
                for h in range(NH):
                    order.append((i, "win", h))
                for q in range(4):
                    order.append((i, "wout", q))
            else:
                for q in range(2):
                    order.append((i, "win", q))
                order.append((i, "wgrp", 0))
                for q in range(2):
                    order.append((i, "wout", q))
            for q in range(8):
                order.append((i, "w1", q))
            for q in range(8):
                order.append((i, "w2", q))
            for q in range(2):
                order.append((i, "gatew", q))
    return order


class Eng:
    def __init__(self, name, h, sem, kind):
        self.name, self.h, self.sem, self.kind = name, h, sem, kind
        self.cnt = 0
        self.seen = {}


class Stream:
    def __init__(self, name, sem):
        self.name, self.sem = name, sem
        self.cnt = 0
        self.kind = "dma"


class Buf:
    __slots__ = ("w", "r", "name", "excl")

    def __init__(self, name="", excl=False):
        self.w = None
        self.r = {}
        self.name = name
        self.excl = excl


class Prog:
    def __init__(self, nc, es):
        self.nc = nc
        self.es = es
        self._nsem = 0
        self.PE = self._eng("pe", nc.tensor, "pe")
        self.ACT = self._eng("act", nc.scalar, "act")
        self.DVE = self._eng("dve", nc.vector, "dve")
        self.POOL = self._eng("pool", nc.gpsimd, "pool")
        self.SP = self._eng("sp", nc.sync, "sp")

    def sem(self, name):
        self._nsem += 1
        return self.es.enter_context(self.nc.semaphore(f"{name}_{self._nsem}"))

    def _eng(self, name, h, kind):
        return Eng(name, h, self.sem("e_" + name), kind)

    def stream(self, name):
        return Stream(name, self.sem("s_" + name))

    def _sync(self, E, reads, writes):
        deps = {}

        def need(src, c):
            if deps.get(src, 0) < c:
                deps[src] = c

        for b in reads:
            if b.w is not None:
                need(*b.w)
            if b.excl:
                for src, c in b.r.items():
                    if src is not E:
                        need(src, c)
        for b in writes:
            if b.w is not None:
                need(*b.w)
            for src, c in b.r.items():
                need(src, c)
        todo = []
        for src, c in deps.items():
            if src is E:
                if E.kind == "pe":
                    continue
            if E.seen.get(src, 0) >= c:
                continue
            todo.append((src, c))
            E.seen[src] = c
        for src, c in todo[:-1]:
            E.h.wait_ge(src.sem, c)
        return todo[-1] if todo else None

    @staticmethod
    def _record(reads, writes, ev):
        src, c = ev
        for b in reads:
            if b.r.get(src, 0) < c:
                b.r[src] = c
        for b in writes:
            b.w = ev
            b.r = {}

    def op(self, E, fn, reads=(), writes=(), inc=True):
        w = self._sync(E, reads, writes)
        ins = fn(E.h)
        if w is not None:
            ins._wait_ge(w[0].sem, w[1])
        if inc:
            E.cnt += 1
            ins.then_inc(E.sem, 1)
            ev = (E, E.cnt)
        else:
            ev = (E, E.cnt + 1)
        self._record(reads, writes, ev)

    def dma(self, E, stream, out, in_, reads=(), writes=()):
        w = self._sync(E, reads, writes)
        ins = E.h.dma_start(out=out, in_=in_)
        if w is not None:
            ins._wait_ge(w[0].sem, w[1])
        ins.then_inc(stream.sem, 16)
        stream.cnt += 16
        self._record(reads, writes, (stream, stream.cnt))

    def barrier(self, engs):
        for E in engs:
            for F in engs:
                if F is E:
                    continue
                if F.cnt > E.seen.get(F, 0):
                    E.h.wait_ge(F.sem, F.cnt)
                    E.seen[F] = F.cnt

    def wait_stream(self, E, stream):
        if stream.cnt > E.seen.get(stream, 0):
            E.h.wait_ge(stream.sem, stream.cnt)
            E.seen[stream] = stream.cnt


def build(layers=(0, 1, 2, 3), nst=NST):
    nc = bass.Bass("TRN2", target_bir_lowering=False)
    xT_d = nc.dram_tensor("xT", [D, S], F32, kind="ExternalInput").ap()
    pT_d = nc.dram_tensor("pT", [DEPTH, PLE, S], F32, kind="ExternalInput").ap()
    cst_d = nc.dram_tensor("cst", [128, NCST], F32, kind="ExternalInput").ap()
    outT_d = nc.dram_tensor("outT", [D, S], F32, kind="ExternalOutput").ap()
    W_d = {}
    for i in layers:
        for kind, (npc, fe) in _layer_kinds(i).items():
            W_d[(i, kind)] = nc.dram_tensor(f"w{i}_{kind}", [npc, 128, fe], F32, kind="ExternalInput").ap()

    G_d, PW_d = {}, {}
    for i in layers:
        if i % 2 == 0:
            G_d[i] = nc.dram_tensor(f"w{i}_gates", [128, GATES_ELEMS], F32, kind="ExternalInput").ap()
        PW_d[i] = nc.dram_tensor(f"w{i}_plew", [128, PLEW_ELEMS], F32, kind="ExternalInput").ap()

    with ExitStack() as es:
        P = Prog(nc, es)
        PE, ACT, DVE, POOL, SP = P.PE, P.ACT, P.DVE, P.POOL, P.SP

        def sb(name, shape, dt):
            return es.enter_context(nc.sbuf_tensor(name, shape, dt))

        xf = sb("xf", [128, KC, T], F32)
        xbA = sb("xbA", [128, KC, T], BF16)
        xbB = sb("xbB", [128, KC, T], BF16)
        ring = [sb(f"ring{k}", [128, SLOT_ELEMS], BF16) for k in range(NSLOT)]
        SCR_F = 16448
        scr = sb("scr", [128, SCR_F], F32)
        scr_b = scr[:].bitcast(BF16)
        NTMP = 6
        tmps = [sb(f"tmp{k}", [128, TB], F32) for k in range(NTMP)]
        stat = [sb(f"stat{k}", [128, TB], F32) for k in range(3)]
        pb = sb("pb", [128, 2, T], BF16)
        gates_t = sb("gates_t", [128, GATES_ELEMS], BF16)
        plew_t = sb("plew_t", [128, PLEW_ELEMS], BF16)
        cs = sb("cs", [128, NCST], F32)
        dc = sb("dc", [128, NDRV], F32)
        kc_t = sb("kc", [128, 8], F32)
        ones = sb("ones", [128, 128], BF16)
        lru_state = sb("lru_state", [128, 2, NH], F32)
        conv_hist = sb("conv_hist", [128, 2, NH, 4], F32)
        pool_hist = sb("pool_hist", [128, 2, KC, HIST], F32)
        invc = sb("invc", [128, 4, HIST], F32)
        psum = [es.enter_context(nc.psum_tensor(f"ps{k}", [128, TB], F32)) for k in range(8)]

        B_xf = [[Buf(f"xf{c}_{b}") for b in range(NB)] for c in range(KC)]
        B_xbA = [[Buf() for b in range(NB)] for c in range(KC)]
        B_xbB = [[Buf() for b in range(NB)] for c in range(KC)]
        B_ring = [Buf() for _ in range(NSLOT)]
        B_tmp = [Buf() for _ in range(NTMP)]
        B_stat = [Buf() for _ in range(3)]
        B_ps = [Buf(excl=True) for _ in range(8)]
        B_pb = Buf()
        B_gates = Buf()
        B_plew = Buf()
        S_g = P.stream("g")
        S_pw = P.stream("pw")
        g_sched = [i for _st in range(nst) for i in layers if i % 2 == 0]
        pw_sched = [i for _st in range(nst) for i in layers]
        pre_state = {"g": 0, "pw": 0}

        def issue_gates():
            k = pre_state["g"]
            if k < len(g_sched):
                P.dma(POOL, S_g, gates_t[:], G_d[g_sched[k]], reads=(), writes=[B_gates])
                pre_state["g"] = k + 1

        def issue_plew():
            k = pre_state["pw"]
            if k < len(pw_sched):
                P.dma(POOL, S_pw, plew_t[:], PW_d[pw_sched[k]], reads=(), writes=[B_plew])
                pre_state["pw"] = k + 1
        B_state = Buf()
        S_ring = [P.stream(f"ring{k}") for k in range(NSLOT)]
        S_x = [P.stream(f"x{c}") for c in range(KC)]
        S_out = [P.stream(f"out{c}") for c in range(KC)]
        S_p = P.stream("p")
        S_xb = P.stream("xb")
        S_c = P.stream("c")

        def xslice(t, c, b):
            return t[:, c, b * TB:(b + 1) * TB]

        rot = {"ps": 0, "tmp": 0, "pool": list(range(8))}

        def next_ps():
            pool = rot["pool"]
            k = pool[rot["ps"] % len(pool)]; rot["ps"] += 1
            return psum[k][:], B_ps[k]

        def next_tmp():
            k = rot["tmp"]; rot["tmp"] = (k + 1) % NTMP
            return tmps[k], B_tmp[k]

        plist = _piece_order(layers, nst)
        ring_state = {"issued": 0, "pos": 0}

        def ring_issue_upto(k):
            while ring_state["issued"] <= min(k, len(plist) - 1):
                j = ring_state["issued"]
                li, kind, pi = plist[j]
                fe = _layer_kinds(li)[kind][1]
                s = j % NSLOT
                P.dma(POOL, S_ring[s], ring[s][:, 0:fe], W_d[(li, kind)][pi], reads=(), writes=[B_ring[s]])
                ring_state["issued"] = j + 1

        def next_piece(li, kind, pi):
            k = ring_state["pos"]
            assert plist[k] == (li, kind, pi), (plist[k], li, kind, pi)
            ring_issue_upto(k + NSLOT - 2)
            ring_state["pos"] = k + 1
            s = k % NSLOT
            return ring[s], B_ring[s]

        def qb_order(nq, head=0, tail=0):
            order = []
            order += [(q, b) for b in range(NB) for q in range(head)]
            order += [(q, b) for q in range(head, nq - tail) for b in range(NB)]
            order += [(q, b) for b in range(NB) for q in range(nq - tail, nq)]
            return order

        def run_pieces(i, kind, order, fn):
            handles = {}
            for q, b in order:
                if q not in handles:
                    handles[q] = next_piece(i, kind, q)
                fn(q, b, *handles[q])

        def act(out, in_, func, reads, writes, bias=None, scale=1.0):
            if bias is None:
                P.op(ACT, lambda h: h.activation(out=out, in_=in_, func=func, scale=scale), reads, writes)
            else:
                P.op(ACT, lambda h: h.activation(out=out, in_=in_, func=func, bias=bias, scale=scale), reads, writes)

        def tt(E, out, a, b, op, reads, writes):
            P.op(E, lambda h: h.tensor_tensor(out=out, in0=a, in1=b, op=op), reads, writes)

        def ts(E, out, in0, s1, s2, op0, op1, reads, writes):
            if op1 is None:
                P.op(E, lambda h: h.tensor_scalar(out=out, in0=in0, scalar1=s1, scalar2=None, op0=op0), reads, writes)
            else:
                P.op(E, lambda h: h.tensor_scalar(out=out, in0=in0, scalar1=s1, scalar2=s2, op0=op0, op1=op1), reads, writes)

        def stt(out, in0, scalar, in1, op0, op1, reads, writes):
            P.op(DVE, lambda h: h.scalar_tensor_tensor(out=out, in0=in0, scalar=scalar, in1=in1, op0=op0, op1=op1),
                 reads, writes)

        def mm_group(ps_ap, ps_buf, terms):
            n = len(terms)
            for idx, (l, r, rb) in enumerate(terms):
                P.op(PE, lambda h, l=l, r=r, idx=idx: h.matmul(ps_ap, lhsT=l, rhs=r, start=(idx == 0), stop=(idx == n - 1)),
                     reads=rb, writes=[ps_buf], inc=(idx == n - 1))

        def cst(i, nm):
            o, w = CST_OFF[(i, nm)]
            return cs[:, o:o + w]

        def drv(i, nm):
            o, w = DRV_OFF[(i, nm)]
            return dc[:, o:o + w]

        KONE, KSIX, KEPS, KZERO = 0, 1, 2, 3

        P.dma(SP, S_c, cs[:], cst_d, reads=(), writes=())
        ring_issue_upto(NSLOT - 3)
        issue_gates()
        issue_plew()
        DVE.h.memset(kc_t[:, KONE:KONE + 1], 1.0)
        DVE.h.memset(kc_t[:, KSIX:KSIX + 1], 1.0 / 16.0)
        DVE.h.memset(kc_t[:, KEPS:KEPS + 1], LN_EPS)
        DVE.h.memset(kc_t[:, KZERO:KZERO + 1], 0.0)
        DVE.h.memset(ones[:], 1.0 / D)
        DVE.h.memset(lru_state[:], 0.0)
        DVE.h.memset(conv_hist[:], 0.0)
        DVE.h.memset(pool_hist[:], 0.0)
        for g, w in enumerate(POOL_W):
            for t in range(HIST):
                DVE.h.memset(invc[:, g, t:t + 1], 1.0 / min(t + 1, w))
        B_init = Buf()
        P.op(DVE, lambda h: h.memset(dc[:], 0.0), reads=(), writes=[B_init])
        P.wait_stream(DVE, S_c)
        P.wait_stream(ACT, S_c)
        for i in layers:
            if i % 2 == 0:
                ts(DVE, drv(i, "hba"), cst(i, "ba"), 0.5, None, ALU.mult, None, [B_init], [B_init])
                ts(DVE, drv(i, "hbx"), cst(i, "bx"), 0.5, None, ALU.mult, None, [B_init], [B_init])
                act(drv(i, "cneg"), cst(i, "lam"), AF.Exp, [B_init], [B_init], scale=-1.0)
                act(drv(i, "cneg"), drv(i, "cneg"), AF.Ln, [B_init], [B_init], bias=kc_t[:, KONE:KONE + 1])
                ts(DVE, drv(i, "hcn"), drv(i, "cneg"), -4.0, None, ALU.mult, None, [B_init], [B_init])
                ts(DVE, drv(i, "cneg"), drv(i, "cneg"), -8.0, None, ALU.mult, None, [B_init], [B_init])
            else:
                tt(DVE, drv(i, "bsc"), cst(i, "bgrp"), cst(i, "pscale"), ALU.mult, [B_init], [B_init])
            ts(DVE, drv(i, "hgb"), cst(i, "gb"), 0.5, None, ALU.mult, None, [B_init], [B_init])
        P.barrier([PE, ACT, DVE])

        STAT_BANKS = {0: (4, 5), 1: (6, 7)}
        ln = {"cnt": [0] * NB, "pend": [], "done": 0}

        def ln_begin():
            rot["pool"] = [0, 1, 2, 3]
            ln["cnt"] = [0] * NB
            ln["done"] = 0

        def emit_stat(b, t1b, Bt1, t2b, Bt2, first, last):
            km, kq = STAT_BANKS[b]
            P.op(PE, lambda h: h.matmul(psum[km][:], lhsT=ones[:], rhs=t1b, start=first, stop=last),
                 reads=[Bt1], writes=[B_ps[km]], inc=False)
            P.op(PE, lambda h: h.matmul(psum[kq][:], lhsT=ones[:], rhs=t2b, start=first, stop=last),
                 reads=[Bt2], writes=[B_ps[kq]], inc=True)

        def ln_finish(i, which, b):
            g_ap, b_ap = cst(i, "g" + which), cst(i, "b" + which)
            km, kq = STAT_BANKS[b]
            ps_m, Bm, ps_q, Bq = psum[km][:], B_ps[km], psum[kq][:], B_ps[kq]
            mean, var, rstd = stat[0][:], stat[1][:], stat[2][:]
            P.op(DVE, lambda h: h.tensor_copy(out=mean, in_=ps_m), [Bm], [B_stat[0]])
            act(var, ps_m, AF.Square, [Bm], [B_stat[1]])
            tt(DVE, var, ps_q, var, ALU.subtract, [Bq, B_stat[1]], [B_stat[1]])
            act(var, var, AF.Ln, [B_stat[1]], [B_stat[1]], bias=kc_t[:, KEPS:KEPS + 1])
            act(rstd, var, AF.Exp, [B_stat[1]], [B_stat[2]], scale=-0.5)
            for c in range(KC):
                t1, Bt1 = next_tmp()
                tt(DVE, t1[:], xslice(xf, c, b), mean, ALU.subtract, [B_xf[c][b], B_stat[0]], [Bt1])
                tt(DVE, t1[:], t1[:], rstd, ALU.mult, [Bt1, B_stat[2]], [Bt1])
                act(xslice(xbA, c, b), t1[:], AF.Identity, [Bt1], [B_xbA[c][b]],
                    bias=b_ap[:, c:c + 1], scale=g_ap[:, c:c + 1])
                ts(POOL, xslice(xf, c, b), t1[:], g_ap[:, c:c + 1], b_ap[:, c:c + 1], ALU.mult, ALU.add,
                   [Bt1], [B_xf[c][b]])
            ln["done"] += 1
            if ln["done"] == NB:
                rot["pool"] = list(range(8))

        def residual_epilogue(ps, Bp, m, b, i, which):
            stt(xslice(xf, m, b), xslice(xf, m, b), ALPHA, ps, ALU.mult, ALU.add, [B_xf[m][b], Bp], [B_xf[m][b]])
            t1, Bt1 = next_tmp()
            t2, Bt2 = next_tmp()
            t1b = t1[:].bitcast(BF16)[:, 0:TB]
            t2b = t2[:].bitcast(BF16)[:, 0:TB]
            act(t1b, xslice(xf, m, b), AF.Copy, [B_xf[m][b]], [Bt1])
            act(t2b, xslice(xf, m, b), AF.Square, [B_xf[m][b]], [Bt2])
            first = ln["cnt"][b] == 0
            last = ln["cnt"][b] == KC - 1
            ln["cnt"][b] += 1
            ln["pend"].append((b, t1b, Bt1, t2b, Bt2, first, last))
            while len(ln["pend"]) > (0 if last else 1):
                emit_stat(*ln["pend"].pop(0))
            if last:
                ln_finish(i, which, b)

        def lru_mixer(i, st, xb_in, B_xb_in):
            slot = i // 2
            o = 0

            def carve_f(n):
                nonlocal o
                a = scr[:, o:o + n]; o += n
                return a
            S1, S2 = 3, 2
            upre = [carve_f(TB + 4) for _ in range(S1)]
            yz = [carve_f(TB) for _ in range(S1)]
            ycp = [carve_f(TB) for _ in range(S1)]
            u_s = [carve_f(TB) for _ in range(S2)]
            tr_s = [carve_f(TB) for _ in range(S2)]
            ti_s = [carve_f(TB) for _ in range(S2)]
            a_s = [carve_f(TB) for _ in range(S2)]
            e2_s = [carve_f(TB) for _ in range(S2)]
            ub_s = []
            for _ in range(S2):
                ub_s.append(scr_b[:, 2 * o:2 * o + TB]); o += TB // 2
            g_t = scr_b[:, 2 * o:2 * o + NH * T]; o += NH * T // 2
            assert o <= SCR_F, o
            Bup = [Buf() for _ in range(S1)]; Byz = [Buf() for _ in range(S1)]; Byc = [Buf() for _ in range(S1)]
            Bu = [Buf() for _ in range(S2)]; Btr = [Buf() for _ in range(S2)]; Bti = [Buf() for _ in range(S2)]
            Ba = [Buf() for _ in range(S2)]; Be2 = [Buf() for _ in range(S2)]; Bub = [Buf() for _ in range(S2)]
            tr_s.append(tmps[0][:]); Btr.append(B_tmp[0])
            ti_s.append(tmps[1][:]); Bti.append(B_tmp[1])
            a_s.append(tmps[2][:]); Ba.append(B_tmp[2])
            e2_s.append(tmps[3][:]); Be2.append(B_tmp[3])
            yz.append(tmps[4][:]); Byz.append(B_tmp[4])
            ycp.append(tmps[5][:]); Byc.append(B_tmp[5])
            SY, SG = 4, 3
            Bg = [[Buf() for _ in range(NB)] for _ in range(NH)]
            cw = cst(i, "convw")
            cb = cst(i, "convb")
            hba, hbx, cneg, hcn = drv(i, "hba"), drv(i, "hbx"), drv(i, "cneg"), drv(i, "hcn")
            gslot, Bgs = gates_t, B_gates
            wcur = {}
            NU = NH * NB

            def stage1(k):
                h, b = divmod(k, NB)
                s1, sy = k % S1, k % SY
                if b == 0:
                    wcur["w"] = next_piece(i, "win", h)
                wslot, Bw = wcur["w"]
                ps_u, Bpu = next_ps()
                mm_group(ps_u, Bpu, [(wslot[:, kc * 256:kc * 256 + 128], xslice(xb_in, kc, b), [Bw, B_xb_in[kc][b]])
                                     for kc in range(KC)])
                ps_y, Bpy = next_ps()
                mm_group(ps_y, Bpy, [(wslot[:, kc * 256 + 128:kc * 256 + 256], xslice(xb_in, kc, b), [Bw, B_xb_in[kc][b]])
                                     for kc in range(KC)])
                act(upre[s1][:, 4:4 + TB], ps_u, AF.Copy, [Bpu], [Bup[s1]])
                act(yz[sy], ps_y, AF.Square, [Bpy], [Byz[sy]], scale=float(np.sqrt(GC2)))
                act(ycp[sy], ps_y, AF.Copy, [Bpy], [Byc[sy]])

            def conv(k):
                h, b = divmod(k, NB)
                s1, s2, sy, sg = k % S1, k % S2, k % SY, k % SG
                up, u_t = upre[s1], u_s[s2]
                if b == 0:
                    P.op(DVE, lambda hh: hh.tensor_copy(out=up[:, 1:4], in_=conv_hist[:, slot, h, 1:4]),
                         [B_state], [Bup[s1]])
                else:
                    sp = (k - 1) % S1
                    P.op(DVE, lambda hh: hh.tensor_copy(out=up[:, 1:4], in_=upre[sp][:, TB + 1:TB + 4]),
                         [Bup[sp]], [Bup[s1]])
                ts(DVE, u_t, up[:, 4:4 + TB], cw[:, 3 * NH + h:3 * NH + h + 1], cb[:, h:h + 1], ALU.mult, ALU.add,
                   [Bup[s1]], [Bu[s2]])
                for kk in (2, 1, 0):
                    stt(u_t, up[:, 1 + kk:1 + kk + TB], cw[:, kk * NH + h:kk * NH + h + 1], u_t, ALU.mult, ALU.add,
                        [Bup[s1], Bu[s2]], [Bu[s2]])
                if b == NB - 1:
                    P.op(DVE, lambda hh: hh.tensor_copy(out=conv_hist[:, slot, h, 1:4], in_=up[:, TB + 1:TB + 4]),
                         [Bup[s1]], [B_state])

            def ubg(k):
                h, b = divmod(k, NB)
                s2, sg = k % S2, k % SG
                act(ub_s[s2], u_s[s2], AF.Copy, [Bu[s2]], [Bub[s2]])
                ps_r, Bpr = next_ps()
                mm_group(ps_r, Bpr, [(gslot[:, h * 128:(h + 1) * 128], ub_s[s2], [Bgs, Bub[s2]])])
                ps_i, Bpi = next_ps()
                mm_group(ps_i, Bpi, [(gslot[:, (NH + h) * 128:(NH + h + 1) * 128], ub_s[s2], [Bgs, Bub[s2]])])
                return ps_r, Bpr, ps_i, Bpi

            def gate_act(k, ps_r, Bpr, ps_i, Bpi):
                h, b = divmod(k, NB)
                s2, sg = k % S2, k % SG
                act(ti_s[sg], ps_i, AF.Tanh, [Bpi], [Bti[sg]], bias=hbx[:, h:h + 1], scale=0.5)
                act(tr_s[sg], ps_r, AF.Tanh, [Bpr], [Btr[sg]], bias=hba[:, h:h + 1], scale=0.5)
                act(a_s[sg], tr_s[sg], AF.Exp, [Btr[sg]], [Ba[sg]], bias=hcn[:, h:h + 1], scale=hcn[:, h:h + 1])
                act(e2_s[sg], tr_s[sg], AF.Exp, [Btr[sg]], [Be2[sg]], bias=cneg[:, h:h + 1], scale=cneg[:, h:h + 1])

            def mid_dve(k):
                s1, s2, sy, sg = k % S1, k % S2, k % SY, k % SG
                stt(ti_s[sg], ti_s[sg], 1.0, u_s[s2], ALU.add, ALU.mult, [Bti[sg], Bu[s2]], [Bti[sg]])
                stt(yz[sy], yz[sy], GC1, ycp[sy], ALU.add, ALU.mult, [Byz[sy], Byc[sy]], [Byz[sy]])
                ts(DVE, e2_s[sg], e2_s[sg], 1.0, -1.0 / 16.0, ALU.min, ALU.mult, [Be2[sg]], [Be2[sg]])

            def tanh_g(k):
                sy = k % SY
                act(yz[sy], yz[sy], AF.Tanh, [Byz[sy]], [Byz[sy]])

            def sqrt_k(k):
                sg = k % SG
                act(e2_s[sg], e2_s[sg], AF.Sqrt, [Be2[sg]], [Be2[sg]], bias=kc_t[:, KSIX:KSIX + 1], scale=1.0)

            def h2_dve(k):
                h, b = divmod(k, NB)
                s1, s2, sy, sg = k % S1, k % S2, k % SY, k % SG
                tr_t, ti_t, a_t, e2_t = tr_s[sg], ti_s[sg], a_s[sg], e2_s[sg]
                if st == 0 and b == 0:
                    P.op(DVE, lambda hh: hh.memset(e2_t[:, 0:1], 0.25), [], [Be2[sg]])
                tt(DVE, ti_t, ti_t, e2_t, ALU.mult, [Bti[sg], Be2[sg]], [Bti[sg]])
                P.op(DVE, lambda hh: hh.tensor_tensor_scan(out=tr_t, data0=a_t, data1=ti_t,
                                                           initial=lru_state[:, slot, h:h + 1],
                                                           op0=ALU.mult, op1=ALU.add),
                     [Ba[sg], Bti[sg], B_state], [Btr[sg]])
                P.op(DVE, lambda hh: hh.tensor_copy(out=lru_state[:, slot, h:h + 1], in_=tr_t[:, TB - 1:TB]),
                     [Btr[sg]], [B_state])
                stt(yz[sy], yz[sy], 1.0, ycp[sy], ALU.add, ALU.mult, [Byz[sy], Byc[sy]], [Byz[sy]])
                tt(DVE, g_t[:, h * T + b * TB:h * T + (b + 1) * TB], tr_t, yz[sy], ALU.mult,
                   [Btr[sg], Byz[sy]], [Bg[h][b]])

            stage1(0)
            pend = {}
            for t in range(NU + 3):
                cs = [t - 3, t - 2] if (0 <= t - 2 < NU and (t - 2) % 2 == 1) else []
                for k in cs:
                    tanh_g(k)
                for k in cs:
                    sqrt_k(k)
                if t < NU:
                    conv(t)
                    pend[t] = ubg(t)
                if 0 <= t - 1 < NU:
                    gate_act(t - 1, *pend.pop(t - 1))
                if 0 <= t - 3 < NU:
                    h2_dve(t - 3)
                if 0 <= t - 1 < NU:
                    mid_dve(t - 1)
                if t + 1 < NU:
                    stage1(t + 1)
            issue_gates()
            ln_begin()

            def wout_fn(q, b, wslot, Bw):
                for mm in range(2):
                    m = q * 2 + mm
                    ps, Bp = next_ps()
                    mm_group(ps, Bp, [(wslot[:, kc * 256 + mm * 128:kc * 256 + (mm + 1) * 128],
                                       g_t[:, kc * T + b * TB:kc * T + (b + 1) * TB], [Bw, Bg[kc][b]])
                                      for kc in range(NH)])
                    residual_epilogue(ps, Bp, m, b, i, "1")
            run_pieces(i, "wout", qb_order(4, tail=2), wout_fn)

        def pool_mixer(i, st, xb_in, B_xb_in):
            slot = i // 2
            o = 0
            def carve_f(n):
                nonlocal o
                a = scr[:, o:o + n]; o += n
                return a
            EXT = HIST + T
            uext = [carve_f(EXT) for _ in range(KC)]
            stmp = [carve_f(EXT) for _ in range(2)]
            zb = scr_b[:, 2 * o:2 * o + KC * T]; o += KC * T // 2
            assert o <= SCR_F, o
            z2b = scr_b[:, 0:KC * T]
            Bue = [Buf() for _ in range(KC)]
            Bst = [Buf(), Buf()]
            Bzb = [[Buf() for _ in range(NB)] for _ in range(KC)]
            bsc, psc = drv(i, "bsc"), cst(i, "pscale")
            for m in range(KC):
                P.op(DVE, lambda hh, m=m: hh.tensor_copy(out=uext[m][:, 0:HIST], in_=pool_hist[:, slot, m, :]),
                     [B_state], [Bue[m]])

            def win_fn(q, b, wslot, Bw):
                for mm in range(4):
                    m = q * 4 + mm
                    ps, Bp = next_ps()
                    mm_group(ps, Bp, [(wslot[:, kc * 512 + mm * 128:kc * 512 + (mm + 1) * 128], xslice(xb_in, kc, b),
                                       [Bw, B_xb_in[kc][b]]) for kc in range(KC)])
                    act(uext[m][:, HIST + b * TB:HIST + (b + 1) * TB], ps, AF.Copy, [Bp], [Bue[m]])

            def window(m):
                g = m // 2
                w = POOL_W[g]
                ue = uext[m]
                P.op(DVE, lambda hh: hh.tensor_copy(out=pool_hist[:, slot, m, :], in_=ue[:, T:T + HIST]),
                     [Bue[m]], [B_state])
                cur, Bcur = ue, Bue[m]
                sh = 1
                k = 0
                while sh < w:
                    dst, Bd = stmp[k % 2], Bst[k % 2]
                    tt(DVE, dst[:, sh:EXT], cur[:, sh:EXT], cur[:, 0:EXT - sh], ALU.add, [Bcur], [Bd])
                    cur, Bcur = dst, Bd
                    sh *= 2
                    k += 1
                if st == 0:
                    tt(DVE, cur[:, HIST:2 * HIST], cur[:, HIST:2 * HIST], invc[:, g, :], ALU.mult, [Bcur], [Bcur])
                    ts(DVE, cur[:, 2 * HIST:EXT], cur[:, 2 * HIST:EXT], 1.0 / w, None, ALU.mult, None, [Bcur], [Bcur])
                    for b in range(NB):
                        tt(DVE, zb[:, m * T + b * TB:m * T + (b + 1) * TB], cur[:, HIST + b * TB:HIST + (b + 1) * TB],
                           ue[:, HIST + b * TB:HIST + (b + 1) * TB], ALU.subtract, [Bcur, Bue[m]], [Bzb[m][b]])
                else:
                    for b in range(NB):
                        stt(zb[:, m * T + b * TB:m * T + (b + 1) * TB], cur[:, HIST + b * TB:HIST + (b + 1) * TB], 1.0 / w,
                            ue[:, HIST + b * TB:HIST + (b + 1) * TB], ALU.mult, ALU.subtract, [Bcur, Bue[m]], [Bzb[m][b]])

            Bz2 = [[Buf() for _ in range(NB)] for _ in range(KC)]

            def grp(g, wslot, Bw):
                for b in range(NB):
                    for mm in range(2):
                        m = 2 * g + mm
                        ps, Bp = next_ps()
                        mm_group(ps, Bp, [(wslot[:, (g * 2 + kk) * 256 + mm * 128:(g * 2 + kk) * 256 + (mm + 1) * 128],
                                           zb[:, (2 * g + kk) * T + b * TB:(2 * g + kk) * T + (b + 1) * TB],
                                           [Bw, Bzb[2 * g + kk][b]]) for kk in range(2)])
                        act(z2b[:, m * T + b * TB:m * T + (b + 1) * TB], ps, AF.Identity, [Bp], [Bz2[m][b]] + Bue[0:4],
                            bias=bsc[:, m:m + 1], scale=psc[:, m:m + 1])

            run_pieces(i, "win", qb_order(2, head=2), win_fn)
            for m in range(4):
                window(m)
            wslot_g, Bw_g = next_piece(i, "wgrp", 0)
            for m in range(4, 8):
                window(m)
            for g in range(4):
                grp(g, wslot_g, Bw_g)
            ln_begin()

            def wout_fn(q, b, wslot, Bw):
                for mm in range(4):
                    m = q * 4 + mm
                    ps, Bp = next_ps()
                    mm_group(ps, Bp, [(wslot[:, kc * 512 + mm * 128:kc * 512 + (mm + 1) * 128],
                                       z2b[:, kc * T + b * TB:kc * T + (b + 1) * TB], [Bw, Bz2[kc][b]])
                                      for kc in range(KC)])
                    residual_epilogue(ps, Bp, m, b, i, "1")
            run_pieces(i, "wout", qb_order(2, tail=2), wout_fn)

        def mlp(i):
            NF = DFF // 128
            hb = scr_b[:, 0:NF * T]
            Bh = [[Buf() for _ in range(NB)] for _ in range(NF)]
            def w1_fn(q, b, wslot, Bw):
                for mm in range(4):
                    j = q * 4 + mm
                    ps, Bp = next_ps()
                    mm_group(ps, Bp, [(wslot[:, kc * 512 + mm * 128:kc * 512 + (mm + 1) * 128], xslice(xbA, kc, b),
                                       [Bw, B_xbA[kc][b]]) for kc in range(KC)])
                    t1, Bt1 = next_tmp()
                    rb = t1[:].bitcast(BF16)[:, 0:TB]
                    act(rb, ps, AF.Relu, [Bp], [Bt1])
                    tt(DVE, hb[:, j * T + b * TB:j * T + (b + 1) * TB], rb, rb, ALU.mult, [Bt1], [Bh[j][b]])
            run_pieces(i, "w1", qb_order(8, head=2), w1_fn)
            ln_begin()

            def w2_fn(q, b, wslot, Bw):
                ps, Bp = next_ps()
                mm_group(ps, Bp, [(wslot[:, j * 128:(j + 1) * 128], hb[:, j * T + b * TB:j * T + (b + 1) * TB],
                                   [Bw, Bh[j][b]]) for j in range(NF)])
                residual_epilogue(ps, Bp, q, b, i, "2")
            run_pieces(i, "w2", qb_order(8, tail=2), w2_fn)

        def ple(i, st):
            hgb = drv(i, "hgb")
            pslot, Bpw = plew_t, B_plew
            def gate_fn(q, b, wslot, Bw):
                if True:
                    for mm in range(4):
                        m = q * 4 + mm
                        ps_g, Bpg = next_ps()
                        mm_group(ps_g, Bpg, [(wslot[:, kc * 512 + mm * 128:kc * 512 + (mm + 1) * 128], xslice(xbA, kc, b),
                                              [Bw, B_xbA[kc][b]]) for kc in range(KC)])
                        ps_p, Bpp = next_ps()
                        mm_group(ps_p, Bpp, [(pslot[:, kk * 1024 + m * 128:kk * 1024 + (m + 1) * 128],
                                              pb[:, kk, b * TB:(b + 1) * TB], [Bpw, B_pb]) for kk in range(2)])
                        t1, Bt1 = next_tmp()
                        act(t1[:], ps_g, AF.Tanh, [Bpg], [Bt1], bias=hgb[:, m:m + 1], scale=0.5)
                        stt(t1[:], t1[:], 1.0, ps_p, ALU.add, ALU.mult, [Bt1, Bpp], [Bt1])
                        stt(xslice(xf, m, b), t1[:], 0.5, xslice(xf, m, b), ALU.mult, ALU.add, [Bt1, B_xf[m][b]], [B_xf[m][b]])
                        if i != layers[-1]:
                            act(xslice(xbB, m, b), xslice(xf, m, b), AF.Copy, [B_xf[m][b]], [B_xbB[m][b]])
            run_pieces(i, "gatew", qb_order(2, head=2), gate_fn)
            issue_plew()

        all_xf = [B_xf[c][b] for c in range(KC) for b in range(NB)]
        for st in range(nst):
            tok = slice(st * T, (st + 1) * T)
            for c in range(KC):
                P.dma(SP, S_x[c], xf[:, c, :], xT_d[c * 128:(c + 1) * 128, tok], reads=(), writes=B_xf[c])
            if st == 0:
                for c in range(KC):
                    for b in range(NB):
                        act(xslice(xbB, c, b), xslice(xf, c, b), AF.Copy, [B_xf[c][b]], [B_xbB[c][b]])
            for i in layers:
                P.dma(POOL, S_p, pb[:], pT_d[i][:, tok].rearrange("(k p) t -> p k t", p=128), reads=(), writes=[B_pb])
                if i % 2 == 0:
                    lru_mixer(i, st, xbB, B_xbB)
                else:
                    pool_mixer(i, st, xbB, B_xbB)
                if i == layers[-1] and st + 1 < nst:
                    ntok = slice((st + 1) * T, (st + 2) * T)
                    P.dma(POOL, S_xb, xbB[:], xT_d[:, ntok].rearrange("(c p) t -> p c t", p=128), reads=(),
                          writes=[B_xbB[c][b] for c in range(KC) for b in range(NB)])
                mlp(i)
                ple(i, st)
            for c in range(KC):
                P.dma(SP, S_out[c], outT_d[c * 128:(c + 1) * 128, tok], xf[:, c, :], reads=B_xf[c], writes=())
        for c in range(KC):
            P.wait_stream(SP, S_out[c])
        assert ring_state["pos"] == len(plist)
    return nc


def _vec(v, n):
    return np.ascontiguousarray(np.asarray(v, np.float32).reshape(n, 128).T)


def _pieces(Wm, cw):
    K, N = Wm.shape
    kc, npc = K // 128, N // cw
    return np.ascontiguousarray(Wm.reshape(kc, 128, npc, cw).transpose(2, 1, 0, 3)).reshape(npc, 128, kc * cw)


def prep_shared(inp, layers=(0, 1, 2, 3)):
    shared = {}
    cstv = np.zeros((128, NCST), np.float32)

    def put(i, nm, arr):
        o, w = CST_OFF[(i, nm)]
        cstv[:, o:o + w] = arr

    for i in layers:
        sl = i // 2
        if i % 2 == 0:
            cw = np.asarray(inp["lru_conv_w"][sl], np.float32)
            put(i, "convw", np.concatenate([_vec(cw[k], NH) for k in range(4)], axis=1))
            put(i, "convb", _vec(inp["lru_conv_b"][sl], NH))
            put(i, "ba", _vec(inp["lru_ba"][sl], NH))
            put(i, "bx", _vec(inp["lru_bx"][sl], NH))
            put(i, "lam", _vec(inp["lru_lambda"][sl], NH))
            win = np.asarray(inp["lru_w_in"][sl], np.float32)
            a = win.reshape(KC, 128, 2, NH, 128).transpose(3, 1, 0, 2, 4)
            shared[f"w{i}_win"] = np.ascontiguousarray(a).reshape(NH, 128, KC * 256)
            wa = np.asarray(inp["lru_wa"][sl], np.float32).transpose(1, 0, 2)
            wx = np.asarray(inp["lru_wx"][sl], np.float32).transpose(1, 0, 2)
            shared[f"w{i}_gates"] = np.ascontiguousarray(np.concatenate([wa, wx], axis=1)).reshape(128, 2 * NH * 128)
            shared[f"w{i}_wout"] = _pieces(np.asarray(inp["lru_w_out"][sl], np.float32), 256)
        else:
            put(i, "bgrp", _vec(inp["pool_b_grp"][sl], KC))
            put(i, "pscale", _vec(inp["pool_scale"][sl], KC))
            shared[f"w{i}_win"] = _pieces(np.asarray(inp["pool_w_in"][sl], np.float32), 512)
            wg = np.asarray(inp["pool_w_grp"][sl], np.float32)
            a = wg.reshape(4, 2, 128, 256).transpose(2, 0, 1, 3)
            shared[f"w{i}_wgrp"] = np.ascontiguousarray(a).reshape(1, 128, 4 * 2 * 256)
            shared[f"w{i}_wout"] = _pieces(np.asarray(inp["pool_w_out"][sl], np.float32), 512)
        put(i, "g1", _vec(inp["ln_mix_g"][i], KC))
        put(i, "b1", _vec(inp["ln_mix_b"][i], KC))
        put(i, "g2", _vec(inp["ln_mlp_g"][i], KC))
        put(i, "b2", _vec(inp["ln_mlp_b"][i], KC))
        put(i, "gb", _vec(inp["ple_gate_b"][i], KC))
        shared[f"w{i}_w1"] = _pieces(np.asarray(inp["mlp_w1"][i], np.float32), 512)
        shared[f"w{i}_w2"] = _pieces(np.asarray(inp["mlp_w2"][i], np.float32), 128)
        shared[f"w{i}_plew"] = _pieces(np.asarray(inp["ple_w"][i], np.float32), 1024)[0]
        shared[f"w{i}_gatew"] = _pieces(np.asarray(inp["ple_gate_w"][i], np.float32), 512)
    shared["cst"] = cstv
    return shared


def make_in_maps(inputs, cores, layers=(0, 1, 2, 3)):
    x = np.asarray(inputs["x"], np.float32)
    p = np.asarray(inputs["p"], np.float32)
    shared = prep_shared(inputs, layers)
    in_maps = []
    for c in cores:
        m = dict(shared)
        m["xT"] = np.ascontiguousarray(x[c].T)
        m["pT"] = np.ascontiguousarray(p[:, c].transpose(0, 2, 1))
        in_maps.append(m)
    return in_maps


def kernel(**inputs):
    x = np.asarray(inputs["x"], np.float32)
    p = np.asarray(inputs["p"], np.float32)
    n = x.shape[0]
    shared = prep_shared(inputs)
    in_maps = []
    for c in range(n):
        m = dict(shared)
        m["xT"] = np.ascontiguousarray(x[c].T)
        m["pT"] = np.ascontiguousarray(p[:, c].transpose(0, 2, 1))
        in_maps.append(m)
    nc = build()
    res = run_bass_kernel_spmd(nc, in_maps, core_ids=list(range(n)))
    out = np.stack([np.asarray(res.results[c]["outT"], np.float32).T for c in range(n)], axis=0)
    return np.ascontiguousarray(out)
```
